# Optimizing a Trainium2 kernel written in Bass

```python
import math
import jax, jax.numpy as jnp
from jax import lax
import numpy as np

D_MODEL = 4096
BATCH = 4
SEQ = 2048
DEPTH = 2

MEM_LEN = 256
EPS = 1e-6

SWA_WIDTH = D_MODEL // 4
GLA_WIDTH = D_MODEL // 4
SSD_WIDTH = D_MODEL // 2
MIX_WIDTH = SWA_WIDTH + GLA_WIDTH + SSD_WIDTH

SWA_HEAD_DIM = 64
SWA_HEADS = SWA_WIDTH // SWA_HEAD_DIM
SWA_KV_HEADS = SWA_HEADS // 8
SWA_KV_WIDTH = SWA_KV_HEADS * SWA_HEAD_DIM
SWA_WINDOW = 128
SWA_BLOCK = 128

REL_BUCKETS = 32
REL_MAX_DIST = 128

GLA_HEADS = 4
GLA_VAL_DIM = GLA_WIDTH // GLA_HEADS
GLA_KEY_DIM = GLA_VAL_DIM // 2
GLA_KEY_WIDTH = GLA_HEADS * GLA_KEY_DIM
GLA_GATE_RANK = 16
GLA_GATE_NORMALIZER = 16.0
GLA_CHUNK = 64

SSD_HEAD_DIM = 64
SSD_HEADS = SSD_WIDTH // SSD_HEAD_DIM
SSD_GROUPS = 8
SSD_STATE = 128
SSD_CONV = 4
SSD_CHUNK = 128
SSD_CONV_CH = SSD_WIDTH + 2 * SSD_GROUPS * SSD_STATE

X_HEADS = 4
X_HEAD_DIM = 128
X_WIDTH = X_HEADS * X_HEAD_DIM

FFN_HIDDEN = 256 * math.ceil(8 * D_MODEL / 3 / 256)

IN_SPLITS = (SWA_WIDTH, SWA_KV_WIDTH, SWA_KV_WIDTH,
             GLA_KEY_WIDTH, GLA_KEY_WIDTH, GLA_WIDTH, GLA_WIDTH, GLA_GATE_RANK,
             SSD_WIDTH, SSD_CONV_CH, SSD_HEADS)
IN_DIM = sum(IN_SPLITS)

kernel_name = "hymba_swa_gla_ssd_xmem_trunk"


def rms_norm(x, g):
    xf = x.astype(jnp.float32)
    y = xf * lax.rsqrt(jnp.mean(xf * xf, axis=-1, keepdims=True) + EPS)
    return (y * g.astype(jnp.float32)).astype(x.dtype)


def t5_bucket(dist):
    n = jnp.maximum(dist, 0)
    max_exact = REL_BUCKETS // 2
    nf = jnp.maximum(n, 1).astype(jnp.float32)
    large = max_exact + (jnp.log(nf / max_exact) / math.log(REL_MAX_DIST / max_exact)
                         * (REL_BUCKETS - max_exact)).astype(jnp.int32)
    large = jnp.minimum(large, REL_BUCKETS - 1)
    return jnp.where(n < max_exact, n, large)


def swa_band_bias(rel_bias, n_blocks):
    i = jnp.arange(SWA_BLOCK, dtype=jnp.int32)[:, None]
    j = jnp.arange(2 * SWA_BLOCK, dtype=jnp.int32)[None, :]
    dist = i + SWA_BLOCK - j
    bias = jnp.transpose(rel_bias[t5_bucket(dist)], (2, 0, 1))
    n = jnp.arange(n_blocks, dtype=jnp.int32)[:, None, None]
    k_abs = n * SWA_BLOCK - SWA_BLOCK + j[None]
    valid = (dist[None] >= 0) & (dist[None] < SWA_WINDOW) & (k_abs >= 0)
    return bias, valid


def swa_mixer(q, k, v, q_gain, k_gain, sinks, out_gain, band_bias, band_valid):
    B_, T, H, dh = q.shape
    nb = T // SWA_BLOCK
    G = H // SWA_KV_HEADS
    q = rms_norm(q, q_gain)
    k = rms_norm(k, k_gain)
    qb = q.reshape(B_, nb, SWA_BLOCK, SWA_KV_HEADS, G, dh)
    pad = ((0, 0), (SWA_BLOCK, 0), (0, 0), (0, 0))
    kb = jnp.pad(k, pad).reshape(B_, nb + 1, SWA_BLOCK, SWA_KV_HEADS, dh)
    vb = jnp.pad(v, pad).reshape(B_, nb + 1, SWA_BLOCK, SWA_KV_HEADS, dh)
    kband = jnp.concatenate([kb[:, :-1], kb[:, 1:]], axis=2)
    vband = jnp.concatenate([vb[:, :-1], vb[:, 1:]], axis=2)
    s = jnp.einsum('bnihgd,bnjhd->bhgnij', qb, kband,
                   preferred_element_type=jnp.float32) * (dh ** -0.5)
    bias = band_bias.astype(jnp.float32).reshape(SWA_KV_HEADS, G, SWA_BLOCK, 2 * SWA_BLOCK)
    s = jnp.where(band_valid[None, None, None], s + bias[None, :, :, None], -jnp.inf)
    sink = sinks.astype(jnp.float32).reshape(SWA_KV_HEADS, G)[None, :, :, None, None, None]
    m = jnp.maximum(jnp.max(s, axis=-1, keepdims=True), sink)
    p = jnp.exp(s - m)
    p = p / (jnp.sum(p, axis=-1, keepdims=True) + jnp.exp(sink - m))
    o = jnp.einsum('bhgnij,bnjhd->bnihgd', p, vband.astype(jnp.float32))
    o = o.reshape(B_, T, H * dh)
    return rms_norm(o, out_gain)


def gla_mixer(q, k, v, r, g_low, w_gk_up, b_gk_up, norm_gain):
    B_, T, _ = q.shape
    H, dk, dv, C = GLA_HEADS, GLA_KEY_DIM, GLA_VAL_DIM, GLA_CHUNK
    N = T // C
    g = jax.nn.log_sigmoid((g_low @ w_gk_up + b_gk_up).astype(jnp.float32)) / GLA_GATE_NORMALIZER
    qf = q.astype(jnp.float32).reshape(B_, N, C, H, dk) * (dk ** -0.5)
    kf = k.astype(jnp.float32).reshape(B_, N, C, H, dk)
    vf = v.astype(jnp.float32).reshape(B_, N, C, H, dv)
    bcum = jnp.cumsum(g.reshape(B_, N, C, H, dk), axis=2)
    q_dec = qf * jnp.exp(bcum)
    att = jnp.einsum('bnihd,bnjhd->bnhij', q_dec, kf * jnp.exp(-bcum))
    causal = jnp.tril(jnp.ones((C, C), dtype=bool))
    att = jnp.where(causal, att, 0.0)
    o_intra = jnp.einsum('bnhij,bnjhv->bnihv', att, vf)
    blast = bcum[:, :, -1]
    dS = jnp.einsum('bnjhd,bnjhv->bnhdv', kf * jnp.exp(blast[:, :, None] - bcum), vf)

    def step(S, inp):
        dec, ds = inp
        return dec[..., None] * S + ds, S

    S0 = jnp.zeros((B_, H, dk, dv), jnp.float32)
    _, S_prev = lax.scan(step, S0, (jnp.moveaxis(jnp.exp(blast), 1, 0), jnp.moveaxis(dS, 1, 0)))
    S_prev = jnp.moveaxis(S_prev, 0, 1)
    o_inter = jnp.einsum('bnihd,bnhdv->bnihv', q_dec, S_prev)
    o = (o_intra + o_inter).reshape(B_, T, H, dv)
    o = rms_norm(o, norm_gain) * jax.nn.silu(r.astype(jnp.float32).reshape(B_, T, H, dv))
    return o.reshape(B_, T, H * dv)


def ssd_mixer(z, xbc, dt_raw, conv_w, conv_b, dt_bias, a_log, d_skip, norm_gain):
    B_, T, _ = xbc.shape
    G, Hg, P, Nst, L = SSD_GROUPS, SSD_HEADS // SSD_GROUPS, SSD_HEAD_DIM, SSD_STATE, SSD_CHUNK
    nc = T // L
    xbc = lax.conv_general_dilated(xbc, conv_w.astype(xbc.dtype), window_strides=(1,),
                                   padding=[(SSD_CONV - 1, 0)],
                                   dimension_numbers=('NWC', 'WIO', 'NWC'),
                                   feature_group_count=SSD_CONV_CH) + conv_b
    xbc = jax.nn.silu(xbc)
    xs, Bm, Cm = jnp.split(xbc, [SSD_WIDTH, SSD_WIDTH + G * Nst], axis=-1)
    x = xs.astype(jnp.float32).reshape(B_, nc, L, G, Hg, P)
    Bm = Bm.astype(jnp.float32).reshape(B_, nc, L, G, Nst)
    Cm = Cm.astype(jnp.float32).reshape(B_, nc, L, G, Nst)
    dt = jax.nn.softplus(dt_raw.astype(jnp.float32) + dt_bias.astype(jnp.float32))
    dt = dt.reshape(B_, nc, L, G, Hg)
    A = -jnp.exp(a_log.astype(jnp.float32)).reshape(G, Hg)
    a_cum = jnp.cumsum(dt * A, axis=2)
    xd = x * dt[..., None]
    diff = a_cum[:, :, :, None] - a_cum[:, :, None, :]
    causal = jnp.tril(jnp.ones((L, L), dtype=bool))[:, :, None, None]
    decay = jnp.exp(jnp.where(causal, diff, -jnp.inf))
    cb = jnp.einsum('bclgn,bcsgn->bclsg', Cm, Bm)
    y_diag = jnp.einsum('bclsgk,bcsgkp->bclgkp', cb[..., None] * decay, xd)
    decay_s = jnp.exp(a_cum[:, :, -1:] - a_cum)
    states = jnp.einsum('bclgn,bclgkp->bcgkpn', Bm, decay_s[..., None] * xd)
    chunk_decay = jnp.exp(a_cum[:, :, -1])

    def step(S, inp):
        dec, st = inp
        return dec[..., None, None] * S + st, S

    S0 = jnp.zeros((B_, G, Hg, P, Nst), jnp.float32)
    _, S_prev = lax.scan(step, S0, (jnp.moveaxis(chunk_decay, 1, 0), jnp.moveaxis(states, 1, 0)))
    S_prev = jnp.moveaxis(S_prev, 0, 1)
    y_off = jnp.einsum('bclgn,bcgkpn->bclgkp', Cm, S_prev) * jnp.exp(a_cum)[..., None]
    y = y_diag + y_off + x * d_skip.astype(jnp.float32).reshape(G, Hg)[:, :, None]
    y = y.reshape(B_, T, SSD_WIDTH) * jax.nn.silu(z.astype(jnp.float32))
    y = rms_norm(y.reshape(B_, T, G, SSD_WIDTH // G), norm_gain.reshape(G, SSD_WIDTH // G))
    return y.reshape(B_, T, SSD_WIDTH)


def cross_attention(hn, memn, w_q, w_k, w_v, w_o, q_gain, k_gain):
    B_, T, _ = hn.shape
    M = memn.shape[1]
    q = rms_norm((hn @ w_q).reshape(B_, T, X_HEADS, X_HEAD_DIM), q_gain)
    k = rms_norm((memn @ w_k).reshape(B_, M, X_HEADS, X_HEAD_DIM), k_gain)
    v = (memn @ w_v).reshape(B_, M, X_HEADS, X_HEAD_DIM)
    s = jnp.einsum('bthd,bmhd->bhtm', q, k, preferred_element_type=jnp.float32) * (X_HEAD_DIM ** -0.5)
    p = jax.nn.softmax(s, axis=-1)
    o = jnp.einsum('bhtm,bmhd->bthd', p, v.astype(jnp.float32)).astype(hn.dtype)
    return o.reshape(B_, T, X_WIDTH) @ w_o


def setup_inputs(seed: int = 0) -> dict:
    key = jax.random.key(seed)
    ks = iter(jax.random.split(key, 48))
    f32 = jnp.float32
    L = DEPTH

    def nrm(shape, scale):
        return scale * jax.random.normal(next(ks), shape, f32)

    def gain(shape):
        return 1.0 + 0.05 * jax.random.normal(next(ks), shape, f32)

    x = nrm((BATCH, SEQ, D_MODEL), 1.0)
    mem = nrm((BATCH, MEM_LEN, D_MODEL), 1.0)
    rel_bias = nrm((REL_BUCKETS, SWA_HEADS), 0.5)
    ln_mix = gain((L, D_MODEL))
    w_in = nrm((L, D_MODEL, IN_DIM), D_MODEL ** -0.5)
    swa_q_gain = gain((L, SWA_HEAD_DIM))
    swa_k_gain = gain((L, SWA_HEAD_DIM))
    swa_sinks = nrm((L, SWA_HEADS), 0.5)
    swa_out_gain = gain((L, SWA_WIDTH))
    gla_w_gk_up = nrm((L, GLA_GATE_RANK, GLA_KEY_WIDTH), GLA_GATE_RANK ** -0.5)
    gla_b_gk_up = nrm((L, GLA_KEY_WIDTH), 0.1)
    gla_norm_gain = gain((L, GLA_VAL_DIM))
    ssd_conv_w = nrm((L, SSD_CONV, 1, SSD_CONV_CH), SSD_CONV ** -0.5)
    ssd_conv_b = nrm((L, SSD_CONV_CH), 0.02)
    dt0 = jnp.exp(jax.random.uniform(next(ks), (L, SSD_HEADS), f32, math.log(1e-3), math.log(1e-1)))
    ssd_dt_bias = dt0 + jnp.log(-jnp.expm1(-dt0))
    ssd_a_log = jnp.log(jax.random.uniform(next(ks), (L, SSD_HEADS), f32, 1.0, 16.0))
    ssd_d = gain((L, SSD_HEADS))
    ssd_norm_gain = gain((L, SSD_WIDTH))
    w_mix_out = nrm((L, MIX_WIDTH, D_MODEL), MIX_WIDTH ** -0.5)
    ln_x = gain((L, D_MODEL))
    ln_mem = gain((L, D_MODEL))
    x_w_q = nrm((L, D_MODEL, X_WIDTH), D_MODEL ** -0.5)
    x_w_k = nrm((L, D_MODEL, X_WIDTH), D_MODEL ** -0.5)
    x_w_v = nrm((L, D_MODEL, X_WIDTH), D_MODEL ** -0.5)
    x_w_o = nrm((L, X_WIDTH, D_MODEL), X_WIDTH ** -0.5)
    x_q_gain = gain((L, X_HEAD_DIM))
    x_k_gain = gain((L, X_HEAD_DIM))
    ln_ffn = gain((L, D_MODEL))
    ffn_w_gate = nrm((L, D_MODEL, FFN_HIDDEN), D_MODEL ** -0.5)
    ffn_w_up = nrm((L, D_MODEL, FFN_HIDDEN), D_MODEL ** -0.5)
    ffn_w_down = nrm((L, FFN_HIDDEN, D_MODEL), FFN_HIDDEN ** -0.5)
    return {"x": x, "mem": mem, "rel_bias": rel_bias, "ln_mix": ln_mix, "w_in": w_in,
            "swa_q_gain": swa_q_gain, "swa_k_gain": swa_k_gain, "swa_sinks": swa_sinks,
            "swa_out_gain": swa_out_gain, "gla_w_gk_up": gla_w_gk_up, "gla_b_gk_up": gla_b_gk_up,
            "gla_norm_gain": gla_norm_gain, "ssd_conv_w": ssd_conv_w, "ssd_conv_b": ssd_conv_b,
            "ssd_dt_bias": ssd_dt_bias, "ssd_a_log": ssd_a_log, "ssd_d": ssd_d,
            "ssd_norm_gain": ssd_norm_gain, "w_mix_out": w_mix_out, "ln_x": ln_x, "ln_mem": ln_mem,
            "x_w_q": x_w_q, "x_w_k": x_w_k, "x_w_v": x_w_v, "x_w_o": x_w_o,
            "x_q_gain": x_q_gain, "x_k_gain": x_k_gain, "ln_ffn": ln_ffn,
            "ffn_w_gate": ffn_w_gate, "ffn_w_up": ffn_w_up, "ffn_w_down": ffn_w_down}


def reference(x, mem, rel_bias, ln_mix, w_in, swa_q_gain, swa_k_gain, swa_sinks, swa_out_gain,
              gla_w_gk_up, gla_b_gk_up, gla_norm_gain, ssd_conv_w, ssd_conv_b, ssd_dt_bias,
              ssd_a_log, ssd_d, ssd_norm_gain, w_mix_out, ln_x, ln_mem, x_w_q, x_w_k, x_w_v,
              x_w_o, x_q_gain, x_k_gain, ln_ffn, ffn_w_gate, ffn_w_up, ffn_w_down):
    B_, T, _ = x.shape
    offsets = np.cumsum(IN_SPLITS)[:-1].tolist()
    band_bias, band_valid = swa_band_bias(rel_bias, T // SWA_BLOCK)
    h = x
    for l in range(DEPTH):
        hn = rms_norm(h, ln_mix[l])
        proj = hn @ w_in[l]
        (a_q, a_k, a_v, b_q, b_k, b_v, b_r, b_glow,
         c_z, c_xbc, c_dt) = jnp.split(proj, offsets, axis=-1)
        y_a = swa_mixer(a_q.reshape(B_, T, SWA_HEADS, SWA_HEAD_DIM),
                        a_k.reshape(B_, T, SWA_KV_HEADS, SWA_HEAD_DIM),
                        a_v.reshape(B_, T, SWA_KV_HEADS, SWA_HEAD_DIM),
                        swa_q_gain[l], swa_k_gain[l], swa_sinks[l], swa_out_gain[l],
                        band_bias, band_valid)
        y_b = gla_mixer(b_q, b_k, b_v, b_r, b_glow, gla_w_gk_up[l], gla_b_gk_up[l], gla_norm_gain[l])
        y_c = ssd_mixer(c_z, c_xbc, c_dt, ssd_conv_w[l], ssd_conv_b[l], ssd_dt_bias[l],
                        ssd_a_log[l], ssd_d[l], ssd_norm_gain[l])
        y = jnp.concatenate([y_a.astype(h.dtype), y_b.astype(h.dtype), y_c.astype(h.dtype)], axis=-1)
        h = h + y @ w_mix_out[l]
        h = h + cross_attention(rms_norm(h, ln_x[l]), rms_norm(mem, ln_mem[l]),
                                x_w_q[l], x_w_k[l], x_w_v[l], x_w_o[l], x_q_gain[l], x_k_gain[l])
        hn = rms_norm(h, ln_ffn[l])
        h = h + (jax.nn.silu(hn @ ffn_w_gate[l]) * (hn @ ffn_w_up[l])) @ ffn_w_down[l]
    return h
```

```python
import math
import numpy as np
from contextlib import ExitStack
import concourse.bass as bass
import concourse.mybir as mybir
from concourse.bass_utils import run_bass_kernel_spmd

F32 = mybir.dt.float32
BF16 = mybir.dt.bfloat16
AF = mybir.ActivationFunctionType
ALU = mybir.AluOpType
AX = mybir.AxisListType

SAME_ENGINE_SYNC = True
EPS = 1e-6
D = 4096
T = 2048
TL = 1024
IN_DIM = 10544
FFN = 11008
NEG = -30000.0


class Buf:
    __slots__ = ("ap", "lw", "rdc", "rdd", "name")

    def __init__(self, ap, name=""):
        self.ap = ap
        self.lw = None
        self.rdc = {}
        self.rdd = []
        self.name = name

    def __getitem__(self, k):
        return self.ap[k]


class _Rec:
    def __getattr__(self, name):
        def f(*a, **k):
            self.__dict__["call"] = (name, a, k)
            return None
        return f


class Ring:
    def __init__(self, bufs):
        self.b = list(bufs)
        self.i = 0

    def next(self):
        b = self.b[self.i % len(self.b)]
        self.i += 1
        return b


class Sched:
    ENG = ["sync", "scalar", "vector", "gpsimd", "tensor"]
    NDS = 12

    def __init__(self, nc, stack):
        self.nc = nc
        self.stack = stack
        self.ins = {e: [] for e in self.ENG}
        self.ncomp = {e: 0 for e in self.ENG}
        self.ndma = {e: 0 for e in self.ENG}
        self.needed = {e: set() for e in self.ENG}
        self.waited_c = {e: {x: -1 for x in self.ENG} for e in self.ENG}
        self.waited_d = {e: set() for e in self.ENG}
        self.csem = {e: stack.enter_context(nc.semaphore("c_" + e)) for e in self.ENG}
        self.dsem = {}
        self.n_sb = 0
        self.rr = 0

    def sbuf(self, shape, dt, name=None):
        self.n_sb += 1
        t = self.stack.enter_context(self.nc.sbuf_tensor(name or f"sb{self.n_sb}", list(shape), dt))
        return Buf(t[:], name or "")

    def psum(self, shape, dt, name=None):
        self.n_sb += 1
        t = self.stack.enter_context(self.nc.psum_tensor(name or f"ps{self.n_sb}", list(shape), dt))
        return Buf(t[:], name or "")

    def _collect(self, eng, reads, writes):
        toks = []
        for b in reads:
            if b.lw is not None:
                toks.append(b.lw)
        for b in writes:
            if b.lw is not None:
                toks.append(b.lw)
            for e_, s_ in b.rdc.items():
                toks.append(("c", e_, s_))
            toks.extend(b.rdd)
        waits = []
        best_c = {}
        for t in toks:
            if t[0] == "c":
                _, e, s = t
                if e == eng and (eng == "tensor" or not SAME_ENGINE_SYNC):
                    continue
                if s <= self.waited_c[eng][e]:
                    continue
                if s > best_c.get(e, -1):
                    best_c[e] = s
            else:
                if t in self.waited_d[eng]:
                    continue
                self.waited_d[eng].add(t)
                waits.append(t)
        for e, s in best_c.items():
            self.waited_c[eng][e] = s
            self.needed[e].add(s)
            waits.append(("c", e, s))
        return waits

    def _commit(self, tok, reads, writes):
        for b in writes:
            b.lw = tok
            b.rdc = {}
            b.rdd = []
        for b in reads:
            if b not in writes:
                if tok[0] == "c":
                    b.rdc[tok[1]] = tok[2]
                else:
                    b.rdd.append(tok)

    def op(self, eng, fn, reads=(), writes=()):
        rec = _Rec()
        fn(rec)
        name_, a_, k_ = rec.call

        def fn(e, name_=name_, a_=a_, k_=k_):
            return getattr(e, name_)(*a_, **k_)
        reads = list(reads)
        writes = list(writes)
        waits = self._collect(eng, reads, writes)
        seq = self.ncomp[eng]
        self.ncomp[eng] += 1
        tok = ("c", eng, seq)
        self.ins[eng].append((waits, fn, tok))
        self._commit(tok, reads, writes)
        return tok

    def dma(self, q, out_ap, in_ap, reads=(), writes=()):
        reads = list(reads)
        writes = list(writes)
        if q not in self.dsem:
            self.dsem[q] = [self.stack.enter_context(self.nc.semaphore(f"d_{q}_{i}"))
                            for i in range(self.NDS)]
        waits = self._collect(q, reads, writes)
        k = self.ndma[q]
        self.ndma[q] += 1
        if k >= self.NDS:
            prev = ("d", q, k - self.NDS)
            if prev not in self.waited_d[q]:
                self.waited_d[q].add(prev)
                waits.append(prev)
        tok = ("d", q, k)

        def fn(e, out_ap=out_ap, in_ap=in_ap):
            return e.dma_start(out=out_ap, in_=in_ap)
        self.ins[q].append((waits, fn, tok))
        self._commit(tok, reads, writes)
        return tok

    def _resolve(self, tok, rank):
        if tok[0] == "c":
            _, e, s = tok
            return self.csem[e], rank[e][s]
        _, q, k = tok
        return self.dsem[q][k % self.NDS], 16 * (k // self.NDS + 1)

    def finish(self, final_tokens):
        nc = self.nc
        fw = []
        for t in final_tokens:
            if t[0] == "c":
                self.needed[t[1]].add(t[2])
            fw.append(t)
        rank = {}
        for e in self.ENG:
            r = {}
            c = 0
            for s in sorted(self.needed[e]):
                c += 1
                r[s] = c
            rank[e] = r
        self.stats = {e: (len(self.ins[e]), len(self.needed[e])) for e in self.ENG}
        with nc.Block() as block:
            def run(e, name):
                for waits, fn, tok in self.ins[name]:
                    for w in waits:
                        sem, val = self._resolve(w, rank)
                        e.wait_ge(sem, val)
                    inst = fn(e)
                    if tok[0] == "c":
                        if tok[2] in rank[name]:
                            inst.then_inc(self.csem[name], 1)
                    else:
                        sem, _ = self._resolve(tok, rank)
                        inst.then_inc(sem, 16)
                if name == "sync":
                    for w in fw:
                        sem, val = self._resolve(w, rank)
                        e.wait_ge(sem, val)

            @block.sync
            def _(e):
                run(e, "sync")

            @block.scalar
            def _(e):
                run(e, "scalar")

            @block.vector
            def _(e):
                run(e, "vector")

            @block.gpsimd
            def _(e):
                run(e, "gpsimd")

            @block.tensor
            def _(e):
                run(e, "tensor")


def aff(S, buf, ap, pattern, cmp, fill, base, cm):
    S.op("gpsimd", lambda e: e.affine_select(out=ap, in_=ap, pattern=pattern, compare_op=cmp,
                                             fill=fill, base=base, channel_multiplier=cm),
         reads=[buf], writes=[buf])


def make_consts(S):
    c = {}
    idf = S.sbuf([128, 128], F32, "identf")
    S.op("gpsimd", lambda e: e.memset(idf[:], 1.0), writes=[idf])
    aff(S, idf, idf[:], [[-1, 128]], ALU.is_equal, 0.0, 0, 1)
    idb = S.sbuf([128, 128], BF16, "identb")
    S.op("vector", lambda e: e.tensor_copy(out=idb[:], in_=idf[:]), reads=[idf], writes=[idb])
    c["idf"], c["idb"] = idf, idb
    return c


class Ctx:
    pass


def setup_common(S, n_ps=4, n_acc=2):
    cx = Ctx()
    cx.S = S
    cx.c = make_consts(S)
    cx.psf = Ring([S.psum([128, 512], F32, f"psf{i}") for i in range(n_ps)])
    cx.psa = Ring([S.psum([128, 512], F32, f"psa{i}") for i in range(n_acc)])
    cx.pst = Ring([S.psum([128, 1024], BF16, f"pst{i}") for i in range(8 - n_ps - n_acc)])
    cx.alt = 0
    return cx


def evac_eng(cx):
    cx.alt += 1
    return "scalar" if cx.alt % 2 else "vector"


def copy_op(S, eng, out_ap, in_ap, reads, writes):
    if eng == "scalar":
        S.op("scalar", lambda e: e.activation(out=out_ap, in_=in_ap, func=AF.Copy), reads=reads, writes=writes)
    else:
        S.op(eng, lambda e: e.tensor_copy(out=out_ap, in_=in_ap), reads=reads, writes=writes)


def rstd_from_ss(S, ss, rs, n, width):
    S.op("scalar", lambda e: e.activation(out=rs[:, 0:n], in_=ss[:, 0:n], func=AF.Sqrt, scale=1.0 / width,
                                          bias=cx_eps(S)[:, 0:1]),
         reads=[ss, cx_eps(S)], writes=[rs])
    S.op("vector", lambda e: e.reciprocal(out=rs[:, 0:n], in_=rs[:, 0:n]), reads=[rs], writes=[rs])


def cx_eps(S):
    if not hasattr(S, "_eps"):
        b = S.sbuf([128, 64], F32, "epsc")
        S.op("vector", lambda e: e.memset(b[:], EPS), writes=[b])
        S._eps = b
    return S._eps


def norm_transpose(cx, x_src_ap, x_reads, xT_big, xT_buf, m, gainT, KC, normed_cols, width):
    S = cx.S
    Dd = KC * 128
    xt = cx.xtile
    xb = cx.xbf
    S.dma("sync", xt[:, 0:Dd], x_src_ap, reads=x_reads, writes=[xt])
    ss = cx.ss
    rs = cx.rs
    S.op("scalar", lambda e: e.activation(out=xb[:, 0:normed_cols], in_=xt[:, 0:normed_cols], func=AF.Square,
                                          accum_out=ss[:, 0:1]), reads=[xt], writes=[xb, ss])
    rstd_from_ss(S, ss, rs, 1, width)
    S.op("vector", lambda e: e.tensor_scalar(out=xb[:, 0:normed_cols], in0=xt[:, 0:normed_cols],
                                             scalar1=rs[:, 0:1], scalar2=None, op0=ALU.mult),
         reads=[xt, rs], writes=[xb])
    if normed_cols < Dd:
        S.op("gpsimd", lambda e: e.tensor_copy(out=xb[:, normed_cols:Dd], in_=xt[:, normed_cols:Dd]),
             reads=[xt], writes=[xb])
    idb = cx.c["idb"]
    for g in range(KC // 8):
        pt = cx.pst.next()
        for j in range(8):
            k = g * 8 + j
            S.op("tensor", lambda e, k=k, j=j, pt=pt: e.transpose(out=pt[:, j * 128:(j + 1) * 128],
                                                                   in_=xb[:, k * 128:(k + 1) * 128],
                                                                   identity=idb[:]),
                 reads=[xb, idb], writes=[pt])
        o_ap = xT_big[:, g * 8:(g + 1) * 8, m * 128:(m + 1) * 128]
        i_ap = pt[:, :].rearrange("p (c t) -> p c t", c=8)
        g_ap = gainT[:, g * 8:(g + 1) * 8].unsqueeze(2).broadcast_to([128, 8, 128])
        S.op("vector", lambda e, o_ap=o_ap, i_ap=i_ap, g_ap=g_ap: e.tensor_tensor(out=o_ap, in0=i_ap, in1=g_ap,
                                                                                  op=ALU.mult),
             reads=[pt, gainT], writes=[xT_buf])


def load_gainT(S, dst, col0, g_ap_2d, n):
    S.dma("sync", dst[:, col0:col0 + n // 128], g_ap_2d, writes=[dst])


def fm(v):
    v = np.asarray(v, dtype=np.float32)
    return np.ascontiguousarray(v.reshape(-1, 128).T)


def linear_tm(cx, xT_big, xT_bufs, KC, TT, w_ap, N, epi, wq="gpsimd", cb_w=512):
    S = cx.S
    for c0 in range(0, N, cb_w):
        ncol = min(cb_w, N - c0)
        tiles = []
        for k in range(KC):
            wt = cx.wring.next()
            S.dma(wq, wt[:, 0:ncol], w_ap[k * 128:(k + 1) * 128, c0:c0 + ncol], writes=[wt])
            tiles.append(wt)
        for m in range(TT):
            ps = cx.psf.next()
            for k in range(KC):
                S.op("tensor", lambda e, k=k, m=m, ps=ps, wt=tiles[k]: e.matmul(
                    ps[:, 0:ncol], lhsT=xT_big[:, k, m * 128:(m + 1) * 128], rhs=wt[:, 0:ncol],
                    start=(k == 0), stop=(k == KC - 1)),
                    reads=[xT_bufs[m], tiles[k]], writes=[ps])
            epi(m, c0, ncol, ps)


def build_A(NCOL=IN_DIM, TT=TL // 128):
    nc = bass.Bass("TRN2", target_bir_lowering=False)
    h = nc.dram_tensor("h", [TT * 128, D], F32, kind="ExternalInput").ap()
    g = nc.dram_tensor("g", [128, D // 128], F32, kind="ExternalInput").ap()
    w = nc.dram_tensor("w", [D, NCOL], F32, kind="ExternalInput").ap()
    out = nc.dram_tensor("out", [TT * 128, NCOL], F32, kind="ExternalOutput").ap()
    with ExitStack() as st:
        S = Sched(nc, st)
        cx = setup_common(S)
        KC = D // 128
        cx.xtile = S.sbuf([128, D], F32, "xtile")
        cx.xbf = S.sbuf([128, D], BF16, "xbf")
        cx.ss = S.sbuf([128, 64], F32, "ss")
        cx.rs = S.sbuf([128, 64], F32, "rs")
        gainT = S.sbuf([128, KC], F32, "gainT")
        load_gainT(S, gainT, 0, g, D)
        xT_t = st.enter_context(nc.sbuf_tensor("xT", [128, KC, TT * 128], BF16))
        xT_big = xT_t[:]
        xT_bufs = [Buf(None, f"xT{m}") for m in range(TT)]
        cx.wring = Ring([S.sbuf([128, 512], BF16, f"w{i}") for i in range(48)])
        ost = Ring([S.sbuf([128, 512], F32, f"o{i}") for i in range(4)])
        for m in range(TT):
            norm_transpose(cx, h[m * 128:(m + 1) * 128, :], [], xT_big, xT_bufs[m], m, gainT, KC, D, D)
        finals = []

        def epi(m, c0, ncol, ps):
            ob = ost.next()
            copy_op(S, evac_eng(cx), ob[:, 0:ncol], ps[:, 0:ncol], [ps], [ob])
            finals.append(S.dma("sync", out[m * 128:(m + 1) * 128, c0:c0 + ncol], ob[:, 0:ncol], reads=[ob]))
        linear_tm(cx, xT_big, xT_bufs, KC, TT, w, NCOL, epi)
        S.finish(finals)
        print("A stats", S.stats)
    return nc


AQ, AK, AV = 0, 512, 576
BQ, BK, BV, BR, BG = 640, 896, 1152, 1664, 2176
CZ, CX, CB, CC, CDT = 2192, 3216, 4240, 4752, 5264
PJW = 5280
NT = T // 128
import os
DO = [c == "1" for c in os.environ.get("DO_MIX", "111")]


def bc(ap_cols, n, w):
    return ap_cols.unsqueeze(2).broadcast_to([128, n, w])


def build_B(NTILES=NT):
    nc = bass.Bass("TRN2", target_bir_lowering=False)
    dram = lambda name, shape: nc.dram_tensor(name, shape, F32, kind="ExternalInput").ap()
    pj = dram("pj", [T, PJW])
    swa_bias = dram("swa_bias", [128, 8, 256])
    swa_qg = dram("swa_qg", [64])
    swa_kg = dram("swa_kg", [64])
    swa_sink = dram("swa_sink", [8])
    gla_w = dram("gla_w", [16, 256])
    gla_b = dram("gla_b", [256])
    gla_g = dram("gla_g", [256])
    conv_w = dram("conv_w", [4, 2048])
    conv_b = dram("conv_b", [2048])
    dt_bias = dram("dt_bias", [16])
    a_log = dram("a_log", [16])
    d_skip = dram("d_skip", [16])
    ssd_g = dram("ssd_g", [1024])
    out = nc.dram_tensor("out", [T, 2048], F32, kind="ExternalOutput").ap()
    with ExitStack() as st:
        S = Sched(nc, st)
        cx = setup_common(S)
        idb, idf = cx.c["idb"], cx.c["idf"]
        V, A_, G = "vector", "scalar", "gpsimd"

        def sb(shape, dt, name):
            return S.sbuf(shape, dt, name)

        def pbload(name, src, n):
            b = sb([128, max(n, 64)], F32, name)
            S.dma("sync", b[:, 0:n], src.partition_broadcast(128), writes=[b])
            return b

        triu = sb([128, 128], F32, "triu")
        S.op(G, lambda e: e.memset(triu[:], 1.0), writes=[triu])
        aff(S, triu, triu[:], [[1, 128]], ALU.is_ge, 0.0, 0, -1)
        rest = sb([128, 128], F32, "rest")
        S.op(G, lambda e: e.memset(rest[:], 1.0), writes=[rest])
        aff(S, rest, rest[:], [[-1, 128]], ALU.is_gt, 0.0, 0, 1)
        ones = sb([128, 128], F32, "ones")
        S.op(G, lambda e: e.memset(ones[:], 1.0), writes=[ones])
        negm = sb([128, 128], F32, "negm")
        S.op(G, lambda e: e.memset(negm[:], 0.0), writes=[negm])
        aff(S, negm, negm[:], [[1, 128]], ALU.is_ge, NEG, 0, -1)
        triu_b = sb([128, 128], BF16, "triu_b")
        S.op(V, lambda e: e.tensor_copy(out=triu_b[:], in_=triu[:]), reads=[triu], writes=[triu_b])
        tri_g = sb([128, 128], F32, "tri_g")
        rest_g = sb([128, 128], F32, "rest_g")
        ncol_g = sb([128, 128], F32, "ncol_g")
        S.op(V, lambda e: e.tensor_scalar(out=tri_g[:], in0=triu[:], scalar1=-1.0 / 16, scalar2=None, op0=ALU.mult),
             reads=[triu], writes=[tri_g])
        S.op(V, lambda e: e.tensor_scalar(out=rest_g[:], in0=rest[:], scalar1=-1.0 / 16, scalar2=None, op0=ALU.mult),
             reads=[rest], writes=[rest_g])
        S.op(V, lambda e: e.memset(ncol_g[:], -1.0 / 16), writes=[ncol_g])
        sel = sb([16, 16, 128], F32, "sel")
        S.op(G, lambda e: e.memset(sel[:], 1.0), writes=[sel])
        aff(S, sel, sel[:], [[-1, 16], [0, 128]], ALU.is_equal, 0.0, 0, 1)

        biasm = sb([128, 8, 256], F32, "biasm")
        S.dma("sync", biasm[:], swa_bias, writes=[biasm])
        qg = pbload("qg", swa_qg, 64)
        kg = pbload("kg", swa_kg, 64)
        sink = pbload("sink", swa_sink, 8)
        glaw = sb([16, 256], F32, "glaw")
        S.dma("sync", glaw[:], gla_w, writes=[glaw])
        glab = pbload("glab", gla_b, 256)
        glag = pbload("glag", gla_g, 256)
        cw = sb([128, 4, 2048], F32, "cw")
        for j in range(4):
            S.dma("sync", cw[:, j, :], conv_w[j, :].partition_broadcast(128), writes=[cw])
        cb_ = pbload("cb_", conv_b, 2048)
        dtb = pbload("dtb", dt_bias, 16)
        alog = pbload("alog", a_log, 16)
        dsk = pbload("dsk", d_skip, 16)
        ssdg = pbload("ssdg", ssd_g, 1024)
        Aneg = sb([128, 16], F32, "Aneg")
        S.op(A_, lambda e: e.activation(out=Aneg[:], in_=alog[:, 0:16], func=AF.Exp), reads=[alog], writes=[Aneg])
        S.op(V, lambda e: e.tensor_scalar(out=Aneg[:], in0=Aneg[:], scalar1=-1.0, scalar2=None, op0=ALU.mult),
             reads=[Aneg], writes=[Aneg])

        Sg = [sb([128, 256], F32, f"Sg{h}") for h in range(2)]
        Sgb = [sb([128, 256], BF16, f"Sgb{h}") for h in range(2)]
        Ss = [sb([128, 256], F32, f"Ss{g}") for g in range(4)]
        Ssb = [sb([128, 256], BF16, f"Ssb{g}") for g in range(4)]
        for b in Sg + Ss:
            S.op(V, lambda e, b=b: e.memset(b[:], 0.0), writes=[b])
        for b in Sgb + Ssb:
            S.op(V, lambda e, b=b: e.memset(b[:], 0.0), writes=[b])
        kT2 = [sb([128, 128], BF16, f"kT2_{i}") for i in range(2)]
        vA = [sb([128, 128], BF16, f"vA{i}") for i in range(2)]

        pa = Ring([sb([128, 640], F32, f"pa{i}") for i in range(2)])
        pb = Ring([sb([128, 1552], F32, f"pb{i}") for i in range(2)])
        pz = Ring([sb([128, 1024], F32, f"pz{i}") for i in range(1)])
        pdt = Ring([sb([128, 64], F32, f"pdt{i}") for i in range(2)])
        xs = [sb([128, 2048], F32, f"xs{j}") for j in range(4)]
        yt = Ring([sb([128, 2048], F32, f"yt{i}") for i in range(1)])
        acc = sb([128, 2048], F32, "acc")
        sq = sb([128, 1024], F32, "sq")
        sm = Ring([sb([128, 64], F32, f"sm{i}") for i in range(24)])
        finals = []

        def small():
            return sm.next()

        for n in range(NTILES):
            r0 = n * 128
            a_t, b_t, z_t, dt_t = pa.next(), pb.next(), pz.next(), pdt.next()
            S.dma("sync", a_t[:], pj[r0:r0 + 128, AQ:AQ + 640], writes=[a_t])
            S.dma("sync", b_t[:], pj[r0:r0 + 128, BQ:BQ + 1552], writes=[b_t])
            S.dma("sync", z_t[:], pj[r0:r0 + 128, CZ:CZ + 1024], writes=[z_t])
            S.dma("sync", dt_t[:, 0:16], pj[r0:r0 + 128, CDT:CDT + 16], writes=[dt_t])
            for j in range(4):
                if n == 0 and j > 0:
                    S.op(V, lambda e, j=j: e.memset(xs[j][0:32, :], 0.0), writes=[xs[j]])
                    S.dma("sync", xs[j][j:128, :], pj[0:128 - j, CX:CX + 2048], writes=[xs[j]])
                else:
                    S.dma("sync", xs[j][:], pj[r0 - j:r0 - j + 128, CX:CX + 2048], writes=[xs[j]])
            y_t = yt.next()

            if DO[0]:
                ssq = small()
                S.op(V, lambda e: e.tensor_tensor(out=sq[:, 0:576], in0=a_t[:, 0:576], in1=a_t[:, 0:576], op=ALU.mult),
                     reads=[a_t], writes=[sq])
                S.op(V, lambda e: e.tensor_reduce(out=ssq[:, 0:9], in_=sq[:, 0:576].rearrange("p (h d) -> p h d", d=64),
                                                  axis=AX.X, op=ALU.add), reads=[sq], writes=[ssq])
                rsq = small()
                rstd_from_ss(S, ssq, rsq, 9, 64)
                qn = sb_once(S, "qn", [128, 640], BF16)
                S.op(V, lambda e: e.tensor_tensor(out=sq[:, 0:576].rearrange("p (h d) -> p h d", d=64),
                                                  in0=a_t[:, 0:576].rearrange("p (h d) -> p h d", d=64),
                                                  in1=bc(rsq[:, 0:9], 9, 64), op=ALU.mult),
                     reads=[a_t, rsq], writes=[sq])
                S.op(V, lambda e: e.scalar_tensor_tensor(out=qn[:, 0:512].rearrange("p (h d) -> p h d", d=64),
                                                         in0=sq[:, 0:512].rearrange("p (h d) -> p h d", d=64),
                                                         scalar=0.125,
                                                         in1=qg[:, 0:64].unsqueeze(1).broadcast_to([128, 8, 64]),
                                                         op0=ALU.mult, op1=ALU.mult),
                     reads=[sq, qg], writes=[qn])
                for dup in range(2):
                    S.op(V, lambda e, dup=dup: e.tensor_tensor(out=qn[:, 512 + dup * 64:576 + dup * 64],
                                                               in0=sq[:, 512:576], in1=kg[:, 0:64], op=ALU.mult),
                         reads=[sq, kg], writes=[qn])
                va = vA[n % 2]
                S.op(G, lambda e, va=va: e.tensor_copy(out=va[:, 0:64], in_=a_t[:, 576:640]), reads=[a_t], writes=[va])
                pt = cx.pst.next()
                for i in range(8):
                    S.op("tensor", lambda e, i=i, pt=pt: e.transpose(out=pt[0:64, i * 128:(i + 1) * 128],
                                                                      in_=qn[:, i * 64:(i + 1) * 64], identity=idb[:]),
                         reads=[qn, idb], writes=[pt])
                qT = sb_once(S, "qT", [128, 1024], BF16)
                S.op(A_, lambda e, pt=pt: e.activation(out=qT[0:64, :], in_=pt[0:64, :], func=AF.Copy), reads=[pt], writes=[qT])
                ptk = cx.pst.next()
                S.op("tensor", lambda e, ptk=ptk: e.transpose(out=ptk[0:64, 0:128], in_=qn[:, 512:576], identity=idb[:]),
                     reads=[qn, idb], writes=[ptk])
                kc = kT2[n % 2]
                kp = kT2[(n + 1) % 2]
                vp = vA[(n + 1) % 2]
                S.op(V, lambda e, ptk=ptk, kc=kc: e.tensor_copy(out=kc[0:64, :], in_=ptk[0:64, 0:128]), reads=[ptk], writes=[kc])
                ops_ = cx.psa.next()
                rden = small()
                for h in range(8):
                    i, p0 = h, 0
                    ps = cx.psf.next()
                    j0 = 128 if n == 0 else 0
                    if n > 0:
                        S.op("tensor", lambda e, ps=ps, i=i, p0=p0, kp=kp: e.matmul(
                            ps[:, 0:128], lhsT=qT[p0:p0 + 64, i * 128:(i + 1) * 128], rhs=kp[p0:p0 + 64, :],
                            start=True, stop=True), reads=[qT, kp], writes=[ps])
                    S.op("tensor", lambda e, ps=ps, i=i, p0=p0, kc=kc: e.matmul(
                        ps[:, 128:256], lhsT=qT[p0:p0 + 64, i * 128:(i + 1) * 128], rhs=kc[p0:p0 + 64, :],
                        start=True, stop=True), reads=[qT, kc], writes=[ps])
                    s_sb = sb_once(S, f"s_sb{h % 2}", [128, 256], F32)
                    S.op(V, lambda e, ps=ps, s_sb=s_sb, h=h, j0=j0: e.tensor_tensor(
                        out=s_sb[:, j0:256], in0=ps[:, j0:256], in1=biasm[:, h, j0:256], op=ALU.add),
                        reads=[ps, biasm], writes=[s_sb])
                    st_ = small()
                    S.op(V, lambda e, s_sb=s_sb, st_=st_, j0=j0: e.tensor_reduce(out=st_[:, 0:1], in_=s_sb[:, j0:256],
                                                                                axis=AX.X, op=ALU.max),
                         reads=[s_sb], writes=[st_])
                    S.op(V, lambda e, st_=st_, h=h: e.tensor_scalar(out=st_[:, 1:2], in0=st_[:, 0:1],
                                                                    scalar1=sink[:, h:h + 1], scalar2=-1.0,
                                                                    op0=ALU.max, op1=ALU.mult),
                         reads=[st_, sink], writes=[st_])
                    p_bf = sb_once(S, f"p_bf{h % 2}", [128, 256], BF16)
                    S.op(A_, lambda e, p_bf=p_bf, s_sb=s_sb, st_=st_, j0=j0: e.activation(
                        out=p_bf[:, j0:256], in_=s_sb[:, j0:256], func=AF.Exp, bias=st_[:, 1:2], scale=1.0,
                        accum_out=st_[:, 2:3]), reads=[s_sb, st_], writes=[p_bf, st_])
                    S.op(A_, lambda e, st_=st_, h=h: e.activation(out=st_[:, 3:4], in_=sink[:, h:h + 1], func=AF.Exp,
                                                                 bias=st_[:, 1:2], scale=1.0),
                         reads=[sink, st_], writes=[st_])
                    S.op(A_, lambda e, st_=st_: e.activation(out=st_[:, 4:5], in_=st_[:, 2:3], func=AF.Identity,
                                                             bias=st_[:, 3:4], scale=1.0), reads=[st_], writes=[st_])
                    S.op(V, lambda e, st_=st_, h=h: e.reciprocal(out=rden[:, h:h + 1], in_=st_[:, 4:5]),
                         reads=[st_], writes=[rden])
                    ptp = cx.pst.next()
                    nk = 1 if n == 0 else 2
                    for kk in range(2 - nk, 2):
                        S.op("tensor", lambda e, kk=kk, ptp=ptp, p_bf=p_bf: e.transpose(
                            out=ptp[:, kk * 128:(kk + 1) * 128], in_=p_bf[:, kk * 128:(kk + 1) * 128], identity=idb[:]),
                            reads=[p_bf, idb], writes=[ptp])
                    pT = sb_once(S, f"pT{h % 2}", [128, 256], BF16)
                    c0_ = (2 - nk) * 128
                    copy_op(S, evac_eng(cx), pT[:, c0_:256], ptp[:, c0_:256], [ptp], [pT])
                    if n > 0:
                        S.op("tensor", lambda e, h=h, pT=pT, vp=vp: e.matmul(
                            ops_[:, h * 64:(h + 1) * 64], lhsT=pT[:, 0:128], rhs=vp[:, 0:64], start=True, stop=False),
                            reads=[pT, vp], writes=[ops_])
                    S.op("tensor", lambda e, h=h, pT=pT, va=va: e.matmul(
                        ops_[:, h * 64:(h + 1) * 64], lhsT=pT[:, 128:256], rhs=va[:, 0:64], start=(n == 0), stop=True),
                        reads=[pT, va], writes=[ops_])
                S.op(V, lambda e, y_t=y_t: e.tensor_tensor(out=y_t[:, 0:512].rearrange("p (h d) -> p h d", d=64),
                                                           in0=ops_[:, 0:512].rearrange("p (h d) -> p h d", d=64),
                                                           in1=bc(rden[:, 0:8], 8, 64), op=ALU.mult),
                     reads=[ops_, rden], writes=[y_t])

            if DO[1]:
                ptg = cx.psf.next()
                S.op("tensor", lambda e, ptg=ptg: e.transpose(out=ptg[0:16, 0:128], in_=b_t[:, 1536:1552], identity=idf[:]),
                     reads=[b_t, idf], writes=[ptg])
                glT = sb_once(S, "glT", [16, 128], F32)
                S.op(V, lambda e, ptg=ptg: e.tensor_copy(out=glT[:], in_=ptg[0:16, 0:128]), reads=[ptg], writes=[glT])
                pg = cx.psf.next()
                S.op("tensor", lambda e, pg=pg: e.matmul(pg[:, 0:256], lhsT=glT[:], rhs=glaw[:], start=True, stop=True),
                     reads=[glT, glaw], writes=[pg])
                sp = sb_once(S, "sp", [128, 256], F32)
                S.op(V, lambda e, pg=pg: e.tensor_tensor(out=sp[:], in0=pg[:, 0:256], in1=glab[:], op=ALU.add),
                     reads=[pg, glab], writes=[sp])
                S.op(A_, lambda e: e.activation(out=sp[:], in_=sp[:], func=AF.Exp, scale=-1.0), reads=[sp], writes=[sp])
                S.op(A_, lambda e: e.activation(out=sp[:], in_=sp[:], func=AF.Ln, bias=cx_one(S)[:, 0:1], scale=1.0),
                     reads=[sp, cx_one(S)], writes=[sp])
                pcum = cx.psf.next()
                S.op("tensor", lambda e, pcum=pcum: e.matmul(pcum[:, 0:256], lhsT=tri_g[:], rhs=sp[:], start=True, stop=True),
                     reads=[tri_g, sp], writes=[pcum])
                S.op("tensor", lambda e, pcum=pcum: e.matmul(pcum[:, 256:512], lhsT=rest_g[:], rhs=sp[:], start=True, stop=True),
                     reads=[rest_g, sp], writes=[pcum])
                eb = sb_once(S, "eb", [128, 768], F32)
                S.op(A_, lambda e, pcum=pcum: e.activation(out=eb[:, 0:256], in_=pcum[:, 0:256], func=AF.Exp),
                     reads=[pcum], writes=[eb])
                S.op(A_, lambda e, pcum=pcum: e.activation(out=eb[:, 256:512], in_=pcum[:, 0:256], func=AF.Exp, scale=-1.0),
                     reads=[pcum], writes=[eb])
                S.op(A_, lambda e, pcum=pcum: e.activation(out=eb[:, 512:768], in_=pcum[:, 256:512], func=AF.Exp),
                     reads=[pcum], writes=[eb])
                qk = sb_once(S, "qk", [128, 768], BF16)
                S.op(V, lambda e: e.scalar_tensor_tensor(out=qk[:, 0:256], in0=b_t[:, 0:256], scalar=128 ** -0.5,
                                                         in1=eb[:, 0:256], op0=ALU.mult, op1=ALU.mult),
                     reads=[b_t, eb], writes=[qk])
                S.op(V, lambda e: e.tensor_tensor(out=qk[:, 256:512], in0=b_t[:, 256:512], in1=eb[:, 256:512], op=ALU.mult),
                     reads=[b_t, eb], writes=[qk])
                S.op(V, lambda e: e.tensor_tensor(out=qk[:, 512:768], in0=b_t[:, 256:512], in1=eb[:, 512:768], op=ALU.mult),
                     reads=[b_t, eb], writes=[qk])
                vg = sb_once(S, "vg", [128, 512], BF16)
                S.op(G, lambda e: e.tensor_copy(out=vg[:], in_=b_t[:, 512:1024]), reads=[b_t], writes=[vg])
                ptq = cx.pst.next()
                for i in range(4):
                    S.op("tensor", lambda e, i=i, ptq=ptq: e.transpose(out=ptq[:, i * 128:(i + 1) * 128],
                                                                        in_=qk[:, i * 128:(i + 1) * 128], identity=idb[:]),
                         reads=[qk, idb], writes=[ptq])
                qkT = sb_once(S, "qkT", [128, 512], BF16)
                S.op(A_, lambda e, ptq=ptq: e.activation(out=qkT[:], in_=ptq[:, 0:512], func=AF.Copy), reads=[ptq], writes=[qkT])
                silr = sb_once(S, "silr", [128, 512], F32)
                S.op(A_, lambda e: e.activation(out=silr[:], in_=b_t[:, 1024:1536], func=AF.Silu), reads=[b_t], writes=[silr])
                for hh in range(2):
                    pat = cx.psf.next()
                    S.op("tensor", lambda e, hh=hh, pat=pat: e.matmul(
                        pat[:, 0:128], lhsT=qkT[:, 256 + hh * 128:384 + hh * 128], rhs=qkT[:, hh * 128:(hh + 1) * 128],
                        start=True, stop=True), reads=[qkT], writes=[pat])
                    attm = sb_once(S, f"attm{hh}", [128, 128], BF16)
                    S.op(V, lambda e, pat=pat, attm=attm: e.tensor_tensor(out=attm[:], in0=pat[:, 0:128], in1=triu[:],
                                                                          op=ALU.mult), reads=[pat, triu], writes=[attm])
                    po = cx.psf.next()
                    S.op("tensor", lambda e, hh=hh, po=po, attm=attm: e.matmul(
                        po[:, 0:256], lhsT=attm[:], rhs=vg[:, hh * 256:(hh + 1) * 256], start=True, stop=False),
                        reads=[attm, vg], writes=[po])
                    S.op("tensor", lambda e, hh=hh, po=po: e.matmul(
                        po[:, 0:256], lhsT=qkT[:, hh * 128:(hh + 1) * 128], rhs=Sgb[hh][:], start=False, stop=True),
                        reads=[qkT, Sgb[hh]], writes=[po])
                    pbl = cx.psf.next()
                    S.op("tensor", lambda e, hh=hh, pbl=pbl: e.matmul(
                        pbl[:, 0:128], lhsT=sp[:, hh * 128:(hh + 1) * 128], rhs=ncol_g[:], start=True, stop=True),
                        reads=[sp, ncol_g], writes=[pbl])
                    ebl = small()
                    S.op(A_, lambda e, pbl=pbl, ebl=ebl: e.activation(out=ebl[:, 0:1], in_=pbl[:, 0:1], func=AF.Exp),
                         reads=[pbl], writes=[ebl])
                    pds = cx.psf.next()
                    S.op("tensor", lambda e, hh=hh, pds=pds: e.matmul(
                        pds[:, 0:256], lhsT=qk[:, 512 + hh * 128:640 + hh * 128], rhs=vg[:, hh * 256:(hh + 1) * 256],
                        start=True, stop=True), reads=[qk, vg], writes=[pds])
                    S.op(V, lambda e, hh=hh, pds=pds, ebl=ebl: e.scalar_tensor_tensor(
                        out=Sg[hh][:], in0=Sg[hh][:], scalar=ebl[:, 0:1], in1=pds[:, 0:256], op0=ALU.mult, op1=ALU.add),
                        reads=[Sg[hh], ebl, pds], writes=[Sg[hh]])
                    S.op(G, lambda e, hh=hh: e.tensor_copy(out=Sgb[hh][:], in_=Sg[hh][:]), reads=[Sg[hh]], writes=[Sgb[hh]])
                    sg_ = small()
                    S.op(A_, lambda e, po=po, sg_=sg_: e.activation(out=sq[:, 0:256], in_=po[:, 0:256], func=AF.Square,
                                                                   accum_out=sg_[:, 0:1]), reads=[po], writes=[sq, sg_])
                    rg_ = small()
                    rstd_from_ss(S, sg_, rg_, 1, 256)
                    S.op(V, lambda e, po=po, rg_=rg_: e.scalar_tensor_tensor(
                        out=sq[:, 0:256], in0=po[:, 0:256], scalar=rg_[:, 0:1], in1=glag[:], op0=ALU.mult, op1=ALU.mult),
                        reads=[po, rg_, glag], writes=[sq])
                    S.op(V, lambda e, hh=hh, y_t=y_t: e.tensor_tensor(
                        out=y_t[:, 512 + hh * 256:768 + hh * 256], in0=sq[:, 0:256], in1=silr[:, hh * 256:(hh + 1) * 256],
                        op=ALU.mult), reads=[sq, silr], writes=[y_t])

            if DO[2]:
                S.op(V, lambda e: e.tensor_tensor(out=acc[:], in0=xs[0][:], in1=cw[:, 3, :], op=ALU.mult),
                     reads=[xs[0], cw], writes=[acc])
                for j in range(1, 4):
                    S.op(G, lambda e, j=j: e.tensor_tensor(out=xs[j][:], in0=xs[j][:], in1=cw[:, 3 - j, :], op=ALU.mult),
                         reads=[xs[j], cw], writes=[xs[j]])
                    S.op(V, lambda e, j=j: e.tensor_tensor(out=acc[:], in0=acc[:], in1=xs[j][:], op=ALU.add),
                         reads=[acc, xs[j]], writes=[acc])
                S.op(V, lambda e: e.tensor_tensor(out=acc[:], in0=acc[:], in1=cb_[:], op=ALU.add), reads=[acc, cb_], writes=[acc])
                xc = sb_once(S, "xc", [128, 2048], F32)
                S.op(A_, lambda e: e.activation(out=xc[:], in_=acc[:], func=AF.Silu), reads=[acc], writes=[xc])
                dts = small()
                S.op(V, lambda e, dts=dts: e.tensor_tensor(out=dts[:, 0:16], in0=dt_t[:, 0:16], in1=dtb[:, 0:16], op=ALU.add),
                     reads=[dt_t, dtb], writes=[dts])
                S.op(A_, lambda e, dts=dts: e.activation(out=dts[:, 0:16], in_=dts[:, 0:16], func=AF.Exp), reads=[dts], writes=[dts])
                S.op(A_, lambda e, dts=dts: e.activation(out=dts[:, 0:16], in_=dts[:, 0:16], func=AF.Ln, bias=cx_one(S)[:, 0:1],
                                                         scale=1.0), reads=[dts, cx_one(S)], writes=[dts])
                S.op(V, lambda e, dts=dts: e.tensor_tensor(out=dts[:, 16:32], in0=dts[:, 0:16], in1=Aneg[:], op=ALU.mult),
                     reads=[dts, Aneg], writes=[dts])
                pc = cx.psf.next()
                for i, m_ in enumerate((triu, rest, ones)):
                    S.op("tensor", lambda e, i=i, m_=m_, pc=pc, dts=dts: e.matmul(
                        pc[:, i * 16:(i + 1) * 16], lhsT=m_[:], rhs=dts[:, 16:32], start=True, stop=True),
                        reads=[m_, dts], writes=[pc])
                pcT = cx.psf.next()
                S.op("tensor", lambda e, pcT=pcT, dts=dts: e.matmul(pcT[0:16, 0:128], lhsT=dts[:, 16:32], rhs=triu[:],
                                                                    start=True, stop=True), reads=[dts, triu], writes=[pcT])
                acT = sb_once(S, "acT", [16, 128], F32)
                S.op(V, lambda e, pcT=pcT: e.tensor_copy(out=acT[:], in_=pcT[0:16, 0:128]), reads=[pcT], writes=[acT])
                ex = small()
                S.op(A_, lambda e, pc=pc, ex=ex: e.activation(out=ex[:, 0:48], in_=pc[:, 0:48], func=AF.Exp), reads=[pc], writes=[ex])
                nac = small()
                S.op(V, lambda e, pc=pc, nac=nac: e.tensor_scalar(out=nac[:, 0:16], in0=pc[:, 0:16], scalar1=-1.0, scalar2=None,
                                                                  op0=ALU.mult), reads=[pc], writes=[nac])
                dd = small()
                S.op(V, lambda e, dd=dd, dts=dts, ex=ex: e.tensor_tensor(out=dd[:, 0:16], in0=dts[:, 0:16], in1=ex[:, 16:32],
                                                                         op=ALU.mult), reads=[dts, ex], writes=[dd])
                xd = sb_once(S, "xd", [128, 1024], BF16)
                xdd = sb_once(S, "xdd", [128, 1024], BF16)
                x3 = xc[:, 0:1024].rearrange("p (h d) -> p h d", d=64)
                S.op(V, lambda e, dts=dts: e.tensor_tensor(out=xd[:].rearrange("p (h d) -> p h d", d=64), in0=x3,
                                                           in1=bc(dts[:, 0:16], 16, 64), op=ALU.mult),
                     reads=[xc, dts], writes=[xd])
                S.op(V, lambda e, dd=dd: e.tensor_tensor(out=xdd[:].rearrange("p (h d) -> p h d", d=64), in0=x3,
                                                         in1=bc(dd[:, 0:16], 16, 64), op=ALU.mult),
                     reads=[xc, dd], writes=[xdd])
                bcb = sb_once(S, "bcb", [128, 1024], BF16)
                S.op(G, lambda e: e.tensor_copy(out=bcb[:], in_=xc[:, 1024:2048]), reads=[xc], writes=[bcb])
                ptb = cx.pst.next()
                for i in range(8):
                    S.op("tensor", lambda e, i=i, ptb=ptb: e.transpose(out=ptb[:, i * 128:(i + 1) * 128],
                                                                        in_=bcb[:, i * 128:(i + 1) * 128], identity=idb[:]),
                         reads=[bcb, idb], writes=[ptb])
                bcT = sb_once(S, "bcT", [128, 1024], BF16)
                S.op(A_, lambda e, ptb=ptb: e.activation(out=bcT[:], in_=ptb[:, 0:1024], func=AF.Copy), reads=[ptb], writes=[bcT])
                ysum = sb_once(S, "ysum", [128, 1024], F32)
                for g in range(4):
                    pcb = cx.psf.next()
                    S.op("tensor", lambda e, g=g, pcb=pcb: e.matmul(
                        pcb[:, 0:128], lhsT=bcT[:, g * 128:(g + 1) * 128], rhs=bcT[:, 512 + g * 128:640 + g * 128],
                        start=True, stop=True), reads=[bcT], writes=[pcb])
                    cbT = sb_once(S, f"cbT{g % 2}", [128, 128], F32)
                    S.op(A_, lambda e, pcb=pcb, cbT=cbT: e.activation(out=cbT[:], in_=pcb[:, 0:128], func=AF.Copy),
                         reads=[pcb], writes=[cbT])
                    pyd = cx.psa.next()
                    for k4 in range(4):
                        h = g * 4 + k4
                        pdc = cx.psf.next()
                        S.op("tensor", lambda e, h=h, pdc=pdc: e.matmul(pdc[:, 0:128], lhsT=sel[:, h, :], rhs=acT[:],
                                                                        start=True, stop=True), reads=[sel, acT], writes=[pdc])
                        dm = sb_once(S, f"dm{h % 2}", [128, 128], F32)
                        S.op(V, lambda e, pdc=pdc, dm=dm: e.tensor_tensor(out=dm[:], in0=pdc[:, 0:128], in1=negm[:], op=ALU.add),
                             reads=[pdc, negm], writes=[dm])
                        S.op(A_, lambda e, dm=dm, h=h, nac=nac: e.activation(out=dm[:], in_=dm[:], func=AF.Exp,
                                                                            bias=nac[:, h:h + 1], scale=1.0),
                             reads=[dm, nac], writes=[dm])
                        MT = sb_once(S, f"MT{h % 2}", [128, 128], BF16)
                        S.op(V, lambda e, dm=dm, MT=MT, cbT=cbT: e.tensor_tensor(out=MT[:], in0=dm[:], in1=cbT[:], op=ALU.mult),
                             reads=[dm, cbT], writes=[MT])
                        S.op("tensor", lambda e, h=h, k4=k4, MT=MT, pyd=pyd: e.matmul(
                            pyd[:, k4 * 64:(k4 + 1) * 64], lhsT=MT[:], rhs=xd[:, h * 64:(h + 1) * 64], start=True, stop=True),
                            reads=[MT, xd], writes=[pyd])
                    pyo = cx.psf.next()
                    S.op("tensor", lambda e, g=g, pyo=pyo: e.matmul(
                        pyo[:, 0:256], lhsT=bcT[:, 512 + g * 128:640 + g * 128], rhs=Ssb[g][:], start=True, stop=True),
                        reads=[bcT, Ssb[g]], writes=[pyo])
                    pst_ = cx.psf.next()
                    S.op("tensor", lambda e, g=g, pst_=pst_: e.matmul(
                        pst_[:, 0:256], lhsT=bcb[:, g * 128:(g + 1) * 128], rhs=xdd[:, g * 256:(g + 1) * 256],
                        start=True, stop=True), reads=[bcb, xdd], writes=[pst_])
                    ys = ysum[:, g * 256:(g + 1) * 256]
                    S.op(V, lambda e, g=g, pyo=pyo, ys=ys, ex=ex: e.tensor_tensor(
                        out=ys.rearrange("p (h d) -> p h d", d=64), in0=pyo[:, 0:256].rearrange("p (h d) -> p h d", d=64),
                        in1=bc(ex[:, g * 4:g * 4 + 4], 4, 64), op=ALU.mult), reads=[pyo, ex], writes=[ysum])
                    S.op(V, lambda e, pyd=pyd, ys=ys: e.tensor_tensor(out=ys, in0=pyd[:, 0:256], in1=ys, op=ALU.add),
                         reads=[pyd, ysum], writes=[ysum])
                    S.op(V, lambda e, g=g, ex=ex: e.tensor_tensor(
                        out=Ss[g][:].rearrange("p (h d) -> p h d", d=64), in0=Ss[g][:].rearrange("p (h d) -> p h d", d=64),
                        in1=bc(ex[:, 32 + g * 4:36 + g * 4], 4, 64), op=ALU.mult), reads=[Ss[g], ex], writes=[Ss[g]])
                    S.op(V, lambda e, g=g, pst_=pst_: e.tensor_tensor(out=Ss[g][:], in0=pst_[:, 0:256], in1=Ss[g][:], op=ALU.add),
                         reads=[pst_, Ss[g]], writes=[Ss[g]])
                    S.op(G, lambda e, g=g: e.tensor_copy(out=Ssb[g][:], in_=Ss[g][:]), reads=[Ss[g]], writes=[Ssb[g]])
                S.op(G, lambda e: e.tensor_tensor(out=xs[1][:, 0:1024].rearrange("p (h d) -> p h d", d=64), in0=x3,
                                                  in1=bc(dsk[:, 0:16], 16, 64), op=ALU.mult), reads=[xc, dsk], writes=[xs[1]])
                S.op(V, lambda e: e.tensor_tensor(out=ysum[:], in0=ysum[:], in1=xs[1][:, 0:1024], op=ALU.add),
                     reads=[ysum, xs[1]], writes=[ysum])
                S.op(A_, lambda e: e.activation(out=xs[2][:, 0:1024], in_=z_t[:], func=AF.Silu), reads=[z_t], writes=[xs[2]])
                S.op(V, lambda e: e.tensor_tensor(out=ysum[:], in0=ysum[:], in1=xs[2][:, 0:1024], op=ALU.mult),
                     reads=[ysum, xs[2]], writes=[ysum])
                S.op(G, lambda e: e.tensor_tensor(out=sq[:], in0=ysum[:], in1=ysum[:], op=ALU.mult), reads=[ysum], writes=[sq])
                sg4 = small()
                S.op(V, lambda e, sg4=sg4: e.tensor_reduce(out=sg4[:, 0:4], in_=sq[:].rearrange("p (g d) -> p g d", d=256),
                                                           axis=AX.X, op=ALU.add), reads=[sq], writes=[sg4])
                rg4 = small()
                rstd_from_ss(S, sg4, rg4, 4, 256)
                S.op(V, lambda e, rg4=rg4: e.tensor_tensor(out=ysum[:].rearrange("p (g d) -> p g d", d=256),
                                                           in0=ysum[:].rearrange("p (g d) -> p g d", d=256),
                                                           in1=bc(rg4[:, 0:4], 4, 256), op=ALU.mult),
                     reads=[ysum, rg4], writes=[ysum])
                S.op(V, lambda e, y_t=y_t: e.tensor_tensor(out=y_t[:, 1024:2048], in0=ysum[:], in1=ssdg[:], op=ALU.mult),
                     reads=[ysum, ssdg], writes=[y_t])
            finals.append(S.dma("scalar", out[r0:r0 + 128, :], y_t[:], reads=[y_t]))
        S.finish(finals)
        print("B stats", S.stats)
    return nc


def sb_once(S, name, shape, dt):
    if not hasattr(S, "_once"):
        S._once = {}
    if name not in S._once:
        S._once[name] = S.sbuf(shape, dt, name)
    return S._once[name]


def cx_one(S):
    if not hasattr(S, "_one"):
        b = S.sbuf([128, 64], F32, "onec")
        S.op("vector", lambda e: e.memset(b[:], 1.0), writes=[b])
        S._one = b
    return S._one


_SPL = (1024, 128, 128, 512, 512, 1024, 1024, 16, 2048, 4096, 32)
_OFF = np.concatenate([[0], np.cumsum(_SPL)]).tolist()


def _t5_bucket(dist):
    n = np.maximum(dist, 0)
    nf = np.maximum(n, 1).astype(np.float32)
    large = 16 + (np.log(nf / 16) / math.log(128 / 16) * 16).astype(np.int32)
    large = np.minimum(large, 31)
    return np.where(n < 16, n, large)


def swa_bias_tiles(rel_bias, p):
    i = np.arange(128)[:, None]
    j = np.arange(256)[None, :]
    dist = i + 128 - j
    valid = (dist >= 0) & (dist < 128)
    g = rel_bias[_t5_bucket(dist)]
    g = g[:, :, p * 8:(p + 1) * 8]
    g = np.where(valid[:, :, None], g, np.float32(NEG))
    return np.ascontiguousarray(np.transpose(g, (0, 2, 1)).astype(np.float32))


def prep_B(inp, l, p, proj_b):
    o = _OFF
    c = lambda a, w: proj_b[:, o[a] + p * w: o[a] + (p + 1) * w]
    xbc = o[9]
    pj = np.concatenate([
        c(0, 512), c(1, 64), c(2, 64), c(3, 256), c(4, 256), c(5, 512), c(6, 512),
        proj_b[:, o[7]:o[7] + 16], c(8, 1024),
        proj_b[:, xbc + p * 1024: xbc + (p + 1) * 1024],
        proj_b[:, xbc + 2048 + p * 512: xbc + 2048 + (p + 1) * 512],
        proj_b[:, xbc + 3072 + p * 512: xbc + 3072 + (p + 1) * 512],
        c(10, 16)], axis=1)
    cwf = inp["ssd_conv_w"][l][:, 0, :]
    csel = lambda a: np.concatenate([a[..., p * 1024:(p + 1) * 1024], a[..., 2048 + p * 512:2048 + (p + 1) * 512],
                                     a[..., 3072 + p * 512:3072 + (p + 1) * 512]], axis=-1)
    f = lambda a: np.ascontiguousarray(a, dtype=np.float32)
    return {
        "pj": f(pj),
        "swa_bias": swa_bias_tiles(inp["rel_bias"], p),
        "swa_qg": f(inp["swa_q_gain"][l]), "swa_kg": f(inp["swa_k_gain"][l]),
        "swa_sink": f(inp["swa_sinks"][l][p * 8:(p + 1) * 8]),
        "gla_w": f(inp["gla_w_gk_up"][l][:, p * 256:(p + 1) * 256]),
        "gla_b": f(inp["gla_b_gk_up"][l][p * 256:(p + 1) * 256]),
        "gla_g": f(inp["gla_norm_gain"][l]),
        "conv_w": f(csel(cwf)), "conv_b": f(csel(inp["ssd_conv_b"][l])),
        "dt_bias": f(inp["ssd_dt_bias"][l][p * 16:(p + 1) * 16]),
        "a_log": f(inp["ssd_a_log"][l][p * 16:(p + 1) * 16]),
        "d_skip": f(inp["ssd_d"][l][p * 16:(p + 1) * 16]),
        "ssd_g": f(inp["ssd_norm_gain"][l][p * 1024:(p + 1) * 1024]),
    }


def assemble_y(y0, y1):
    return np.concatenate([y0[:, 0:512], y1[:, 0:512], y0[:, 512:1024], y1[:, 512:1024],
                           y0[:, 1024:2048], y1[:, 1024:2048]], axis=1)


def ffn_passes(nchunks, per):
    res, f = [], 0
    while f < nchunks:
        n = min(per, nchunks - f)
        res.append((f, n))
        f += n
    return res


def build_C(TT=TL // 128, FF=FFN, do_ffn=True):
    nc = bass.Bass("TRN2", target_bir_lowering=False)
    dram = lambda name, shape: nc.dram_tensor(name, shape, F32, kind="ExternalInput").ap()
    NTK = TT * 128
    KC = D // 128
    h_in = dram("h", [NTK, D])
    y_in = dram("y", [NTK, D])
    mem = dram("mem", [256, D])
    g_swa = dram("g_swa", [128, 8])
    g_x = dram("g_x", [128, KC])
    g_mem = dram("g_mem", [128, KC])
    g_ffn = dram("g_ffn", [128, KC])
    xqg = dram("xqg", [128])
    xkg = dram("xkg", [128])
    w_mix = dram("w_mix", [D, D])
    w_q = dram("w_q", [D, 512])
    w_k = dram("w_k", [D, 512])
    w_v = dram("w_v", [D, 512])
    w_o = dram("w_o", [512, D])
    w_g = dram("w_g", [D, FF])
    w_u = dram("w_u", [D, FF])
    w_d = dram("w_d", [FF, D])
    hout = nc.dram_tensor("hout", [NTK, D], F32, kind="ExternalOutput").ap()
    h1 = nc.dram_tensor("h1s", [NTK, D], F32, kind="Internal").ap()
    h2 = nc.dram_tensor("h2s", [NTK, D], F32, kind="Internal").ap()
    with ExitStack() as st:
        S = Sched(nc, st)
        cx = setup_common(S)
        idb = cx.c["idb"]
        V, A_, G = "vector", "scalar", "gpsimd"
        cx.xtile = S.sbuf([128, D], F32, "xtile")
        cx.xbf = S.sbuf([128, D], BF16, "xbf")
        cx.ss = S.sbuf([128, 64], F32, "ss")
        cx.rs = S.sbuf([128, 64], F32, "rs")
        xT_t = st.enter_context(nc.sbuf_tensor("xT", [128, KC, NTK], BF16))
        xT_big = xT_t[:]
        xT_bufs = [Buf(None, f"xT{m}") for m in range(TT)]
        cx.wring = Ring([S.sbuf([128, 512], BF16, f"w{i}") for i in range(48)])
        ost = Ring([S.sbuf([128, 512], F32, f"o{i}") for i in range(4)])
        hpr = Ring([S.sbuf([128, 512], F32, f"hp{i}") for i in range(4)])
        sm = Ring([S.sbuf([128, 64], F32, f"sm{i}") for i in range(16)])
        gT_mix = S.sbuf([128, KC], F32, "gT_mix")
        S.op(V, lambda e: e.memset(gT_mix[:], 1.0), writes=[gT_mix])
        S.dma("sync", gT_mix[:, 0:8], g_swa, writes=[gT_mix])
        gT = {}
        for nm, src in (("x", g_x), ("mem", g_mem), ("ffn", g_ffn)):
            gT[nm] = S.sbuf([128, KC], F32, "gT_" + nm)
            S.dma("sync", gT[nm][:], src, writes=[gT[nm]])
        qgb = S.sbuf([128, 128], F32, "qgb")
        kgb = S.sbuf([128, 128], F32, "kgb")
        S.dma("sync", qgb[:], xqg.partition_broadcast(128), writes=[qgb])
        S.dma("sync", kgb[:], xkg.partition_broadcast(128), writes=[kgb])
        db = {nm: {(m, cb): Buf(None, f"{nm}{m}_{cb}") for m in range(TT) for cb in range(8)}
              for nm in ("h1", "h2")}
        finals = []

        def resid_epi(src_ap, src_bufs, dst_ap, dst_bufs, final=False):
            def epi(m, c0, ncol, ps):
                cb = c0 // 512
                hp = hpr.next()
                rd = [src_bufs[(m, cb)]] if src_bufs is not None else []
                S.dma("sync", hp[:, 0:ncol], src_ap[m * 128:(m + 1) * 128, c0:c0 + ncol], reads=rd, writes=[hp])
                ob = ost.next()
                S.op(V, lambda e: e.tensor_tensor(out=ob[:, 0:ncol], in0=ps[:, 0:ncol], in1=hp[:, 0:ncol], op=ALU.add),
                     reads=[ps, hp], writes=[ob])
                wr = [dst_bufs[(m, cb)]] if dst_bufs is not None else []
                t = S.dma("scalar", dst_ap[m * 128:(m + 1) * 128, c0:c0 + ncol], ob[:, 0:ncol], reads=[ob], writes=wr)
                if final:
                    finals.append(t)
            return epi

        for m in range(TT):
            norm_transpose(cx, y_in[m * 128:(m + 1) * 128, :], [], xT_big, xT_bufs[m], m, gT_mix, KC, 1024, 1024)
        linear_tm(cx, xT_big, xT_bufs, KC, TT, w_mix, D, resid_epi(h_in, None, h1, db["h1"]))

        kT = S.sbuf([128, 4, 256], BF16, "kT")
        vM = S.sbuf([128, 2, 512], BF16, "vM")
        sqb = S.sbuf([128, 512], F32, "sqb")
        qnb = S.sbuf([128, 512], BF16, "qnb")
        qT4 = S.sbuf([128, 512], BF16, "qT4")
        pTb = S.sbuf([128, 256], BF16, "pTb")
        pbf = S.sbuf([128, 256], BF16, "pbf")
        obf = S.sbuf([128, 512], BF16, "obf")
        oT_t = st.enter_context(nc.sbuf_tensor("oT", [128, 4, NTK], BF16))
        oT_big = oT_t[:]
        oT_bufs = [Buf(None, f"oT{m}") for m in range(TT)]

        def qk_norm(ps, gain_b, scale):
            ssq = sm.next()
            S.op(A_, lambda e: e.activation(out=sqb[:], in_=ps[:, 0:512], func=AF.Square), reads=[ps], writes=[sqb])
            S.op(V, lambda e: e.tensor_reduce(out=ssq[:, 0:4], in_=sqb[:].rearrange("p (h d) -> p h d", d=128),
                                              axis=AX.X, op=ALU.add), reads=[sqb], writes=[ssq])
            rsq = sm.next()
            rstd_from_ss(S, ssq, rsq, 4, 128)
            S.op(V, lambda e: e.tensor_tensor(out=sqb[:].rearrange("p (h d) -> p h d", d=128),
                                              in0=ps[:, 0:512].rearrange("p (h d) -> p h d", d=128),
                                              in1=bc(rsq[:, 0:4], 4, 128), op=ALU.mult), reads=[ps, rsq], writes=[sqb])
            S.op(V, lambda e: e.scalar_tensor_tensor(out=qnb[:].rearrange("p (h d) -> p h d", d=128),
                                                     in0=sqb[:].rearrange("p (h d) -> p h d", d=128), scalar=scale,
                                                     in1=gain_b[:, 0:128].unsqueeze(1).broadcast_to([128, 4, 128]),
                                                     op0=ALU.mult, op1=ALU.mult), reads=[sqb, gain_b], writes=[qnb])

        for mt in range(2):
            norm_transpose(cx, mem[mt * 128:(mt + 1) * 128, :], [], xT_big, xT_bufs[mt], mt, gT["mem"], KC, D, D)

        def k_epi(mt, c0, ncol, ps):
            qk_norm(ps, kgb, 1.0)
            pt = cx.pst.next()
            for hd in range(4):
                S.op("tensor", lambda e: e.transpose(out=pt[:, hd * 128:(hd + 1) * 128], in_=qnb[:, hd * 128:(hd + 1) * 128],
                                                     identity=idb[:]), reads=[qnb, idb], writes=[pt])
            S.op(A_, lambda e: e.activation(out=kT[:, :, mt * 128:(mt + 1) * 128],
                                            in_=pt[:, 0:512].rearrange("p (h t) -> p h t", h=4), func=AF.Copy),
                 reads=[pt], writes=[kT])

        def v_epi(mt, c0, ncol, ps):
            S.op(V, lambda e: e.tensor_copy(out=vM[:, mt, :], in_=ps[:, 0:512]), reads=[ps], writes=[vM])
        linear_tm(cx, xT_big, xT_bufs, KC, 2, w_k, 512, k_epi)
        linear_tm(cx, xT_big, xT_bufs, KC, 2, w_v, 512, v_epi)

        for m in range(TT):
            norm_transpose(cx, h1[m * 128:(m + 1) * 128, :], [db["h1"][(m, cb)] for cb in range(8)],
                           xT_big, xT_bufs[m], m, gT["x"], KC, D, D)

        def q_epi(m, c0, ncol, ps):
            qk_norm(ps, qgb, 128 ** -0.5)
            pt = cx.pst.next()
            for hd in range(4):
                S.op("tensor", lambda e: e.transpose(out=pt[:, hd * 128:(hd + 1) * 128], in_=qnb[:, hd * 128:(hd + 1) * 128],
                                                     identity=idb[:]), reads=[qnb, idb], writes=[pt])
            S.op(A_, lambda e: e.activation(out=qT4[:], in_=pt[:, 0:512], func=AF.Copy), reads=[pt], writes=[qT4])
            po = cx.psa.next()
            rden = sm.next()
            for hd in range(4):
                pss = cx.psf.next()
                S.op("tensor", lambda e: e.matmul(pss[:, 0:256], lhsT=qT4[:, hd * 128:(hd + 1) * 128], rhs=kT[:, hd, :],
                                                  start=True, stop=True), reads=[qT4, kT], writes=[pss])
                st_ = sm.next()
                S.op(V, lambda e: e.tensor_reduce(out=st_[:, 0:1], in_=pss[:, 0:256], axis=AX.X, op=ALU.max),
                     reads=[pss], writes=[st_])
                S.op(V, lambda e: e.tensor_scalar(out=st_[:, 1:2], in0=st_[:, 0:1], scalar1=-1.0, scalar2=None, op0=ALU.mult),
                     reads=[st_], writes=[st_])
                S.op(A_, lambda e: e.activation(out=pbf[:], in_=pss[:, 0:256], func=AF.Exp, bias=st_[:, 1:2], scale=1.0,
                                                accum_out=st_[:, 2:3]), reads=[pss, st_], writes=[pbf, st_])
                S.op(A_, lambda e: e.activation(out=st_[:, 4:5], in_=st_[:, 2:3], func=AF.Identity), reads=[st_], writes=[st_])
                S.op(V, lambda e: e.reciprocal(out=rden[:, hd:hd + 1], in_=st_[:, 4:5]), reads=[st_], writes=[rden])
                ptp = cx.pst.next()
                for mt in range(2):
                    S.op("tensor", lambda e: e.transpose(out=ptp[:, mt * 128:(mt + 1) * 128], in_=pbf[:, mt * 128:(mt + 1) * 128],
                                                         identity=idb[:]), reads=[pbf, idb], writes=[ptp])
                copy_op(S, evac_eng(cx), pTb[:], ptp[:, 0:256], [ptp], [pTb])
                for mt in range(2):
                    S.op("tensor", lambda e: e.matmul(po[:, hd * 128:(hd + 1) * 128], lhsT=pTb[:, mt * 128:(mt + 1) * 128],
                                                      rhs=vM[:, mt, hd * 128:(hd + 1) * 128], start=(mt == 0), stop=(mt == 1)),
                         reads=[pTb, vM], writes=[po])
            S.op(V, lambda e: e.tensor_tensor(out=obf[:].rearrange("p (h d) -> p h d", d=128),
                                              in0=po[:, 0:512].rearrange("p (h d) -> p h d", d=128),
                                              in1=bc(rden[:, 0:4], 4, 128), op=ALU.mult), reads=[po, rden], writes=[obf])
            pt2 = cx.pst.next()
            for hd in range(4):
                S.op("tensor", lambda e: e.transpose(out=pt2[:, hd * 128:(hd + 1) * 128], in_=obf[:, hd * 128:(hd + 1) * 128],
                                                     identity=idb[:]), reads=[obf, idb], writes=[pt2])
            S.op(A_, lambda e: e.activation(out=oT_big[:, :, m * 128:(m + 1) * 128],
                                            in_=pt2[:, 0:512].rearrange("p (h t) -> p h t", h=4), func=AF.Copy),
                 reads=[pt2], writes=[oT_bufs[m]])
        linear_tm(cx, xT_big, xT_bufs, KC, TT, w_q, 512, q_epi)
        last_dst = (h2, db["h2"]) if do_ffn else (hout, None)
        linear_tm(cx, oT_big, oT_bufs, 4, TT, w_o, D, resid_epi(h1, db["h1"], last_dst[0], last_dst[1], final=not do_ffn))

        if do_ffn:
            for m in range(TT):
                norm_transpose(cx, h2[m * 128:(m + 1) * 128, :], [db["h2"][(m, cb)] for cb in range(8)],
                               xT_big, xT_bufs[m], m, gT["ffn"], KC, D, D)
            PER = 11
            aT_t = st.enter_context(nc.sbuf_tensor("aT", [128, PER, NTK], BF16))
            aT_big = aT_t[:]
            aT_buf = Buf(None, "aT")
            sgb = Ring([S.sbuf([128, 512], F32, f"sg{i}") for i in range(2)])
            passes = ffn_passes(FF // 128, PER)
            NH = NTK // 512
            for pi, (f0, nf) in enumerate(passes):
                for fi in range(nf):
                    f = f0 + fi
                    wt = {}
                    for nm, wsrc in (("g", w_g), ("u", w_u)):
                        tl = []
                        for j in range(KC // 4):
                            rt = cx.wring.next()
                            S.dma(G, rt[:, 0:512].rearrange("p (k c) -> p k c", k=4),
                                  wsrc[j * 512:(j + 1) * 512, f * 128:(f + 1) * 128].rearrange("(k p) c -> p k c", p=128),
                                  writes=[rt])
                            tl.append(rt)
                        wt[nm] = tl
                    for half in range(NH):
                        pp = {}
                        for nm in ("g", "u"):
                            ps = cx.psf.next()
                            for k in range(KC):
                                rt = wt[nm][k // 4]
                                S.op("tensor", lambda e: e.matmul(ps[:, 0:512], lhsT=rt[:, (k % 4) * 128:(k % 4 + 1) * 128],
                                                                  rhs=xT_big[:, k, half * 512:(half + 1) * 512],
                                                                  start=(k == 0), stop=(k == KC - 1)),
                                     reads=[rt] + xT_bufs[half * 4:(half + 1) * 4], writes=[ps])
                            pp[nm] = ps
                        sg = sgb.next()
                        S.op(A_, lambda e: e.activation(out=sg[:], in_=pp["g"][:, 0:512], func=AF.Silu),
                             reads=[pp["g"]], writes=[sg])
                        S.op(V, lambda e: e.tensor_tensor(out=aT_big[:, fi, half * 512:(half + 1) * 512], in0=pp["u"][:, 0:512],
                                                          in1=sg[:], op=ALU.mult), reads=[pp["u"], sg], writes=[aT_buf])
                last = pi == len(passes) - 1
                dst = (hout, None) if last else (h2, db["h2"])
                linear_tm(cx, aT_big, [aT_buf] * TT, nf, TT, w_d[f0 * 128:(f0 + nf) * 128, :], D,
                          resid_epi(h2, db["h2"], dst[0], dst[1], final=last))
        S.finish(finals)
        print("C stats", S.stats)
    return nc


def prep_C(inp, l, b, p, h_core, y_core):
    f = lambda a: np.ascontiguousarray(a, dtype=np.float32)
    return {
        "h": f(h_core), "y": f(y_core), "mem": f(inp["mem"][b]),
        "g_swa": fm(inp["swa_out_gain"][l]), "g_x": fm(inp["ln_x"][l]), "g_mem": fm(inp["ln_mem"][l]),
        "g_ffn": fm(inp["ln_ffn"][l]), "xqg": f(inp["x_q_gain"][l]), "xkg": f(inp["x_k_gain"][l]),
        "w_mix": f(inp["w_mix_out"][l]), "w_q": f(inp["x_w_q"][l]), "w_k": f(inp["x_w_k"][l]),
        "w_v": f(inp["x_w_v"][l]), "w_o": f(inp["x_w_o"][l]), "w_g": f(inp["ffn_w_gate"][l]),
        "w_u": f(inp["ffn_w_up"][l]), "w_d": f(inp["ffn_w_down"][l]),
    }


_PROGS = {}


def _prog(name):
    if name not in _PROGS:
        _PROGS[name] = {"A": build_A, "B": build_B, "C": build_C}[name]()
    return _PROGS[name]


def kernel(**inputs):
    inp = {k: np.asarray(v) for k, v in inputs.items()}
    B_ = 4
    cores = list(range(8))
    h = np.ascontiguousarray(inp["x"], dtype=np.float32).reshape(B_, 2, TL, D).copy()
    for l in range(2):
        gA = fm(inp["ln_mix"][l])
        wA = np.ascontiguousarray(inp["w_in"][l], dtype=np.float32)
        maps = [{"h": h[b, p], "g": gA, "w": wA} for b in range(B_) for p in range(2)]
        res = run_bass_kernel_spmd(_prog("A"), maps, core_ids=cores).results
        proj = [np.concatenate([res[2 * b]["out"], res[2 * b + 1]["out"]], axis=0) for b in range(B_)]
        maps = [prep_B(inp, l, p, proj[b]) for b in range(B_) for p in range(2)]
        res = run_bass_kernel_spmd(_prog("B"), maps, core_ids=cores).results
        ys = [assemble_y(res[2 * b]["out"], res[2 * b + 1]["out"]) for b in range(B_)]
        maps = [prep_C(inp, l, b, p, h[b, p], ys[b][p * TL:(p + 1) * TL]) for b in range(B_) for p in range(2)]
        res = run_bass_kernel_spmd(_prog("C"), maps, core_ids=cores).results
        for b in range(B_):
            for p in range(2):
                h[b, p] = res[2 * b + p]["hout"]
    return h.reshape(B_, T, D).astype(np.float32)
```
